# Optimizing a Trainium2 kernel written in Bass

```python
import math
import jax, jax.numpy as jnp
from jax import lax
import numpy as np

D_MODEL = 1024
BATCH = 16
SEQ = 2048
DEPTH = 4

HEAD_DIM = 64
ROT_DIM = HEAD_DIM // 4
ROPE_THETA = 500000.0
Q_BLOCK = 128
EPS = 1e-6

FOX_HEADS = D_MODEL // (2 * HEAD_DIM)
FOX_WIDTH = FOX_HEADS * HEAD_DIM
FOX_FGATE_BIAS = 3.0
DIFF_V_DIM = 2 * HEAD_DIM
DIFF_HEADS = D_MODEL // (2 * DIFF_V_DIM)
DIFF_WIDTH = DIFF_HEADS * DIFF_V_DIM
DSA_HEADS = D_MODEL // (2 * HEAD_DIM)
DSA_NOPE = HEAD_DIM - ROT_DIM
DSA_WIDTH = DSA_HEADS * HEAD_DIM
DSA_KV_RANK = 128
IDX_HEADS = 4
IDX_DIM = 64
IDX_TOPK_MAX = 256
SB_HEADS = D_MODEL // (2 * HEAD_DIM)
SB_WIDTH = SB_HEADS * HEAD_DIM
D_FF = 2816
CONV_WIDTH = 3

EVEN_SPLITS = (FOX_WIDTH, FOX_WIDTH, FOX_WIDTH, FOX_HEADS,
               DIFF_WIDTH, DIFF_WIDTH, DIFF_WIDTH)
ODD_SPLITS = (DSA_WIDTH, DSA_KV_RANK, ROT_DIM, IDX_HEADS * IDX_DIM, IDX_HEADS, IDX_DIM,
              SB_WIDTH, SB_WIDTH, SB_WIDTH)
EVEN_IN = sum(EVEN_SPLITS)
ODD_IN = sum(ODD_SPLITS)
N_EVEN = (DEPTH + 1) // 2
N_ODD = DEPTH // 2

kernel_name = "hybrid_fox_diff_dsa_stickbreak_convffn"


def rms_norm(x, g):
    xf = x.astype(jnp.float32)
    y = xf * lax.rsqrt(jnp.mean(xf * xf, axis=-1, keepdims=True) + EPS)
    return (y * g.astype(jnp.float32)).astype(x.dtype)


def split_cols(z, sizes):
    cuts = []
    acc = 0
    for s in sizes[:-1]:
        acc += s
        cuts.append(acc)
    return jnp.split(z, cuts, axis=-1)


def rope_tables(positions):
    inv_freq = ROPE_THETA ** (-jnp.arange(0, ROT_DIM, 2, dtype=jnp.float32) / ROT_DIM)
    ang = positions.astype(jnp.float32)[..., None] * inv_freq
    return jnp.cos(ang), jnp.sin(ang)


def apply_partial_rope(x, cos, sin):
    half = ROT_DIM // 2
    xr = x[..., :ROT_DIM].astype(jnp.float32)
    x1, x2 = xr[..., :half], xr[..., half:]
    rot = jnp.concatenate([x1 * cos - x2 * sin, x2 * cos + x1 * sin], axis=-1)
    return jnp.concatenate([rot.astype(x.dtype), x[..., ROT_DIM:]], axis=-1)


def sweep(block_fn, seq_len):
    out = lax.map(block_fn, jnp.arange(seq_len // Q_BLOCK))
    out = jnp.moveaxis(out, 0, 1)
    return out.reshape((out.shape[0], seq_len) + out.shape[3:])


def gather_rows(table, idx):
    return jax.vmap(lambda t, i: jnp.take(t, i, axis=0))(table, idx)


def fox_attention(q, k, v, log_f):
    seq_len = q.shape[1]
    scale = HEAD_DIM ** -0.5
    c = jnp.transpose(jnp.cumsum(log_f, axis=1), (0, 2, 1))
    kpos = jnp.arange(seq_len)

    def block(i):
        start = i * Q_BLOCK
        qpos = start + jnp.arange(Q_BLOCK)
        qb = lax.dynamic_slice_in_dim(q, start, Q_BLOCK, axis=1)
        cb = lax.dynamic_slice_in_dim(c, start, Q_BLOCK, axis=2)
        s = jnp.einsum('bqhd,bkhd->bhqk', qb, k).astype(jnp.float32) * scale
        s = s + (cb[..., :, None] - c[..., None, :])
        s = jnp.where(qpos[:, None] >= kpos[None, :], s, -jnp.inf)
        p = jax.nn.softmax(s, axis=-1).astype(v.dtype)
        return jnp.einsum('bhqk,bkhd->bqhd', p, v)

    return sweep(block, seq_len)


def diff_attention(q1, q2, k1, k2, v, lam):
    seq_len = q1.shape[1]
    scale = HEAD_DIM ** -0.5
    kpos = jnp.arange(seq_len)

    def block(i):
        start = i * Q_BLOCK
        qpos = start + jnp.arange(Q_BLOCK)
        mask = qpos[:, None] >= kpos[None, :]
        q1b = lax.dynamic_slice_in_dim(q1, start, Q_BLOCK, axis=1)
        q2b = lax.dynamic_slice_in_dim(q2, start, Q_BLOCK, axis=1)
        s1 = jnp.einsum('bqhd,bkhd->bhqk', q1b, k1).astype(jnp.float32) * scale
        s2 = jnp.einsum('bqhd,bkhd->bhqk', q2b, k2).astype(jnp.float32) * scale
        p1 = jax.nn.softmax(jnp.where(mask, s1, -jnp.inf), axis=-1)
        p2 = jax.nn.softmax(jnp.where(mask, s2, -jnp.inf), axis=-1)
        p = (p1 - lam * p2).astype(v.dtype)
        return jnp.einsum('bhqk,bkhd->bqhd', p, v)

    return sweep(block, seq_len)


def stick_breaking_attention(q, k, v):
    seq_len = q.shape[1]
    scale = HEAD_DIM ** -0.5
    kpos = jnp.arange(seq_len)

    def block(i):
        start = i * Q_BLOCK
        qpos = start + jnp.arange(Q_BLOCK)
        mask = kpos[None, :] < qpos[:, None]
        qb = lax.dynamic_slice_in_dim(q, start, Q_BLOCK, axis=1)
        z = jnp.einsum('bqhd,bkhd->bhqk', qb, k).astype(jnp.float32) * scale
        log1m = jnp.where(mask, jax.nn.log_sigmoid(-z), 0.0)
        after = lax.cumsum(log1m, axis=3, reverse=True) - log1m
        a = jnp.where(mask, jnp.exp(jax.nn.log_sigmoid(z) + after), 0.0)
        return jnp.einsum('bhqk,bkhd->bqhd', a.astype(v.dtype), v)

    return sweep(block, seq_len)


def dsa_attention(q_abs, q_rope, c_kv, k_rope, iq, iw, ik, top_k):
    seq_len = q_abs.shape[1]
    scale = HEAD_DIM ** -0.5
    kpos = jnp.arange(seq_len)

    def block(i):
        start = i * Q_BLOCK
        qpos = start + jnp.arange(Q_BLOCK)
        iqb = lax.dynamic_slice_in_dim(iq, start, Q_BLOCK, axis=1)
        iwb = lax.dynamic_slice_in_dim(iw, start, Q_BLOCK, axis=1)
        logits = jnp.einsum('bqhd,bkd->bqhk', iqb, ik).astype(jnp.float32)
        score = jnp.einsum('bqhk,bqh->bqk', jax.nn.relu(logits), iwb.astype(jnp.float32))
        score = jnp.where(kpos[None, None, :] <= qpos[None, :, None], score, -jnp.inf)
        _, sel = lax.top_k(score, top_k)
        valid = sel <= qpos[None, :, None]
        c_sel = gather_rows(c_kv, sel)
        kr_sel = gather_rows(k_rope, sel)
        qab = lax.dynamic_slice_in_dim(q_abs, start, Q_BLOCK, axis=1)
        qrb = lax.dynamic_slice_in_dim(q_rope, start, Q_BLOCK, axis=1)
        s = (jnp.einsum('bqhr,bqkr->bqhk', qab, c_sel)
             + jnp.einsum('bqhd,bqkd->bqhk', qrb, kr_sel)).astype(jnp.float32) * scale
        s = jnp.where(valid[:, :, None, :], s, -jnp.inf)
        p = jax.nn.softmax(s, axis=-1).astype(c_kv.dtype)
        return jnp.einsum('bqhk,bqkr->bqhr', p, c_sel)

    return sweep(block, seq_len)


def even_mixer(h, cos, sin, w_in, fgate_b, lq1, lk1, lq2, lk2, subln, w_out, layer_idx):
    b, s, _ = h.shape
    aq, ak, av, af, bq, bk, bv = split_cols(h @ w_in, EVEN_SPLITS)
    log_f = jax.nn.log_sigmoid((af + fgate_b).astype(jnp.float32))
    o_a = fox_attention(aq.reshape(b, s, FOX_HEADS, HEAD_DIM),
                        ak.reshape(b, s, FOX_HEADS, HEAD_DIM),
                        av.reshape(b, s, FOX_HEADS, HEAD_DIM), log_f)
    cs, sn = cos[:, :, None, :], sin[:, :, None, :]
    bq = bq.reshape(b, s, DIFF_HEADS, 2, HEAD_DIM)
    bk = bk.reshape(b, s, DIFF_HEADS, 2, HEAD_DIM)
    q1 = apply_partial_rope(bq[..., 0, :], cs, sn)
    q2 = apply_partial_rope(bq[..., 1, :], cs, sn)
    k1 = apply_partial_rope(bk[..., 0, :], cs, sn)
    k2 = apply_partial_rope(bk[..., 1, :], cs, sn)
    lam_init = 0.8 - 0.6 * math.exp(-0.3 * layer_idx)
    lam = (jnp.exp(jnp.sum(lq1.astype(jnp.float32) * lk1.astype(jnp.float32)))
           - jnp.exp(jnp.sum(lq2.astype(jnp.float32) * lk2.astype(jnp.float32))) + lam_init)
    o_b = diff_attention(q1, q2, k1, k2, bv.reshape(b, s, DIFF_HEADS, DIFF_V_DIM), lam)
    o_b = rms_norm(o_b, subln) * (1.0 - lam_init)
    o = jnp.concatenate([o_a.reshape(b, s, FOX_WIDTH), o_b.reshape(b, s, DIFF_WIDTH)], axis=-1)
    return o @ w_out


def odd_mixer(h, cos, sin, w_in, kv_norm, w_uk, w_uv, w_out, top_k):
    b, s, _ = h.shape
    cq, ckv, ckr, ciq, ciw, cik, sq, sk, sv = split_cols(h @ w_in, ODD_SPLITS)
    cs, sn = cos[:, :, None, :], sin[:, :, None, :]
    cq = apply_partial_rope(cq.reshape(b, s, DSA_HEADS, HEAD_DIM), cs, sn)
    q_rope, q_nope = cq[..., :ROT_DIM], cq[..., ROT_DIM:]
    c_kv = rms_norm(ckv, kv_norm)
    k_rope = apply_partial_rope(ckr, cos, sin)
    q_abs = jnp.einsum('bshn,rhn->bshr', q_nope, w_uk)
    iq = apply_partial_rope(ciq.reshape(b, s, IDX_HEADS, IDX_DIM), cs, sn)
    ik = apply_partial_rope(cik, cos, sin)
    iw = ciw * (IDX_HEADS ** -0.5)
    o_lat = dsa_attention(q_abs, q_rope, c_kv, k_rope, iq, iw, ik, top_k)
    o_c = jnp.einsum('bshr,rhd->bshd', o_lat, w_uv).reshape(b, s, DSA_WIDTH)
    o_d = stick_breaking_attention(sq.reshape(b, s, SB_HEADS, HEAD_DIM),
                                   sk.reshape(b, s, SB_HEADS, HEAD_DIM),
                                   sv.reshape(b, s, SB_HEADS, HEAD_DIM)).reshape(b, s, SB_WIDTH)
    return jnp.concatenate([o_c, o_d], axis=-1) @ w_out


def conv_ffn(h, w_in, conv_w, conv_b, w_out):
    u = h @ w_in
    u = lax.conv_general_dilated(u, conv_w[:, None, :], window_strides=(1,),
                                 padding=[(CONV_WIDTH - 1, 0)],
                                 dimension_numbers=('NWC', 'WIO', 'NWC'),
                                 feature_group_count=u.shape[-1]) + conv_b
    g, v = jnp.split(u, 2, axis=-1)
    return (jax.nn.silu(g) * v) @ w_out


def setup_inputs(seed: int = 0) -> dict:
    key = jax.random.key(seed)
    ks = jax.random.split(key, 24)
    f32 = jnp.float32

    def nrm(k, shape, scale):
        return jax.random.normal(k, shape, f32) * scale

    return {
        "x": nrm(ks[0], (BATCH, SEQ, D_MODEL), 1.0),
        "positions": jnp.broadcast_to(jnp.arange(SEQ, dtype=jnp.int32), (BATCH, SEQ)),
        "attn_norm": 1.0 + nrm(ks[1], (DEPTH, D_MODEL), 0.02),
        "ffn_norm": 1.0 + nrm(ks[2], (DEPTH, D_MODEL), 0.02),
        "final_norm": 1.0 + nrm(ks[3], (D_MODEL,), 0.02),
        "ev_w_in": nrm(ks[4], (N_EVEN, D_MODEL, EVEN_IN), D_MODEL ** -0.5),
        "ev_fgate_b": FOX_FGATE_BIAS + nrm(ks[5], (N_EVEN, FOX_HEADS), 0.1),
        "ev_lambda_q1": nrm(ks[6], (N_EVEN, HEAD_DIM), 0.1),
        "ev_lambda_k1": nrm(ks[7], (N_EVEN, HEAD_DIM), 0.1),
        "ev_lambda_q2": nrm(ks[8], (N_EVEN, HEAD_DIM), 0.1),
        "ev_lambda_k2": nrm(ks[9], (N_EVEN, HEAD_DIM), 0.1),
        "ev_subln": 1.0 + nrm(ks[10], (N_EVEN, DIFF_V_DIM), 0.02),
        "ev_w_out": nrm(ks[11], (N_EVEN, D_MODEL, D_MODEL), D_MODEL ** -0.5),
        "od_w_in": nrm(ks[12], (N_ODD, D_MODEL, ODD_IN), D_MODEL ** -0.5),
        "od_kv_norm": 1.0 + nrm(ks[13], (N_ODD, DSA_KV_RANK), 0.02),
        "od_w_uk": nrm(ks[14], (N_ODD, DSA_KV_RANK, DSA_HEADS, DSA_NOPE), DSA_KV_RANK ** -0.5),
        "od_w_uv": nrm(ks[15], (N_ODD, DSA_KV_RANK, DSA_HEADS, HEAD_DIM), DSA_KV_RANK ** -0.5),
        "od_w_out": nrm(ks[16], (N_ODD, D_MODEL, D_MODEL), D_MODEL ** -0.5),
        "ffn_w_in": nrm(ks[17], (DEPTH, D_MODEL, 2 * D_FF), D_MODEL ** -0.5),
        "ffn_conv_w": nrm(ks[18], (DEPTH, CONV_WIDTH, 2 * D_FF), CONV_WIDTH ** -0.5),
        "ffn_conv_b": nrm(ks[19], (DEPTH, 2 * D_FF), 0.02),
        "ffn_w_out": nrm(ks[20], (DEPTH, D_FF, D_MODEL), D_FF ** -0.5),
    }


def reference(x, positions, attn_norm, ffn_norm, final_norm,
              ev_w_in, ev_fgate_b, ev_lambda_q1, ev_lambda_k1, ev_lambda_q2, ev_lambda_k2,
              ev_subln, ev_w_out,
              od_w_in, od_kv_norm, od_w_uk, od_w_uv, od_w_out,
              ffn_w_in, ffn_conv_w, ffn_conv_b, ffn_w_out):
    cos, sin = rope_tables(positions)
    top_k = min(IDX_TOPK_MAX, x.shape[1] // 4)
    for layer in range(DEPTH):
        j = layer // 2
        h = rms_norm(x, attn_norm[layer])
        if layer % 2 == 0:
            x = x + even_mixer(h, cos, sin, ev_w_in[j], ev_fgate_b[j],
                               ev_lambda_q1[j], ev_lambda_k1[j], ev_lambda_q2[j], ev_lambda_k2[j],
                               ev_subln[j], ev_w_out[j], layer)
        else:
            x = x + odd_mixer(h, cos, sin, od_w_in[j], od_kv_norm[j], od_w_uk[j], od_w_uv[j],
                              od_w_out[j], top_k)
        h = rms_norm(x, ffn_norm[layer])
        x = x + conv_ffn(h, ffn_w_in[layer], ffn_conv_w[layer], ffn_conv_b[layer], ffn_w_out[layer])
    return rms_norm(x, final_norm)
```

```python
import math
from contextlib import ExitStack

import numpy as np
import concourse.bass as bass
import concourse.mybir as mybir
from concourse.bass_utils import run_bass_kernel_spmd

F32 = mybir.dt.float32
BF16 = mybir.dt.bfloat16
I32 = mybir.dt.int32
AF = mybir.ActivationFunctionType
ALU = mybir.AluOpType
AX = mybir.AxisListType

D = 1024
S = 2048
NT = 16
DEPTH = 4
EPS = 1e-6
EVEN_IN = 3080
ODD_IN = 2516
DFF = 2816
NCH = 22
TOPK = 256
NEG = -30000.0
NBIS = 17
PAGE = 1024
OTOK_PAGES = 32
ARENA_PAGES = 66
NPAGES = OTOK_PAGES + ARENA_PAGES


class Buf:
    __slots__ = ("name", "w", "r", "excl")

    def __init__(self, name, excl=False):
        self.name = name
        self.w = None
        self.r = {}
        self.excl = excl


class Prog:
    ENG = ("pe", "act", "dve", "pool", "sp")
    EMAP = {"pe": "tensor", "act": "scalar", "dve": "vector", "pool": "gpsimd", "sp": "sync"}

    def __init__(self, nc, stack):
        self.nc = nc
        self.stack = stack
        self.lists = {e: [] for e in self.ENG}
        self.cnt = {e: 0 for e in self.ENG}
        self.sems = {e: stack.enter_context(nc.semaphore("s_" + e)) for e in self.ENG}
        self.waited = {e: {} for e in self.ENG}
        self.nsem = len(self.ENG)
        self.ninstr = 0

    def newsem(self, name):
        self.nsem += 1
        return self.stack.enter_context(self.nc.semaphore(name))

    def _deps(self, eng, reads, writes):
        deps = {}
        for b in reads:
            w = b.w
            if w is not None and deps.get(w[0], 0) < w[1]:
                deps[w[0]] = w[1]
            if b.excl:
                for k, v in b.r.items():
                    if deps.get(k, 0) < v:
                        deps[k] = v
        for b in writes:
            w = b.w
            if w is not None and deps.get(w[0], 0) < w[1]:
                deps[w[0]] = w[1]
            for k, v in b.r.items():
                if deps.get(k, 0) < v:
                    deps[k] = v
        out = []
        wd = self.waited[eng]
        own = self.sems[eng] if eng == "pe" else None
        for k, v in deps.items():
            if k is own or wd.get(k, 0) >= v:
                continue
            wd[k] = v
            out.append((k, v))
        return out

    def op(self, eng, fn, reads=(), writes=()):
        waits = self._deps(eng, reads, writes)
        self.cnt[eng] += 1
        t = self.cnt[eng]
        sem = self.sems[eng]
        self.lists[eng].append((waits, fn, sem, 1))
        self.ninstr += 1
        for b in reads:
            if b.r.get(sem, 0) < t:
                b.r[sem] = t
        for b in writes:
            b.w = (sem, t)
            b.r = {}
        return t

    def dma(self, eng, fn, semh, reads=(), writes=()):
        waits = self._deps(eng, reads, writes)
        if semh.sem is None:
            semh.sem = self.newsem("d%d" % self.nsem)
        semh.cnt += 16
        self.lists[eng].append((waits, fn, semh.sem, 16))
        self.ninstr += 1
        for b in reads:
            b.r[semh.sem] = semh.cnt
        for b in writes:
            b.w = (semh.sem, semh.cnt)
            b.r = {}

    def final_wait(self, eng, sem, val):
        self.lists[eng].append(([(sem, val)], None, None, 0))

    def emit(self, block):
        def mk(ename):
            lst = self.lists[ename]

            def body(e):
                for waits, fn, sem, inc in lst:
                    for (s, v) in waits:
                        e.wait_ge(s, v)
                    if fn is not None:
                        fn(e).then_inc(sem, inc)
            return body
        for ename in self.ENG:
            getattr(block, self.EMAP[ename])(mk(ename))


class SemH:
    __slots__ = ("sem", "cnt")

    def __init__(self):
        self.sem = None
        self.cnt = 0


def flat(*items):
    out = []
    for it in items:
        if it is None:
            continue
        if isinstance(it, Buf):
            out.append(it)
        elif isinstance(it, Reg):
            out.extend(it.bufs)
        else:
            out.extend(flat(*it))
    return out


class Reg:
    def __init__(self, kb, off, n):
        self.kb = kb
        self.off = off
        self.n = n
        self.bufs = kb.pages[off // PAGE:(off + n + PAGE - 1) // PAGE]
        self.semh = kb.semcache.setdefault((off, n), SemH())

    def ap(self, dtype=BF16):
        a = self.kb.arena[:, self.off // 2:(self.off + self.n) // 2]
        if dtype == BF16:
            return a
        return a.bitcast(dtype)

    def sub(self, off, n):
        return Reg(self.kb, self.off + off, n)


class Bump:
    def __init__(self, kb, lo_page, hi_page):
        self.kb = kb
        self.pos = lo_page * PAGE
        self.hi = hi_page * PAGE

    def alloc(self, n, align=PAGE):
        p = (self.pos + align - 1) // align * align
        assert p + n <= self.hi, ("arena overflow", p, n, self.hi)
        self.pos = p + n
        return Reg(self.kb, p, n)


WEIGHT_SPECS = [
    ("attn_norm", [4, D]), ("ffn_norm", [4, D]), ("final_norm", [D]),
    ("ev_w_in", [2, D, EVEN_IN]), ("ev_fgate_b", [2, 8]),
    ("ev_lambda_q1", [2, 64]), ("ev_lambda_k1", [2, 64]), ("ev_lambda_q2", [2, 64]), ("ev_lambda_k2", [2, 64]),
    ("ev_subln", [2, 128]), ("ev_w_out", [2, D, D]),
    ("od_w_in", [2, D, ODD_IN]), ("od_kv_norm", [2, 128]), ("od_w_uk", [2, 128, 8, 48]),
    ("od_w_uv", [2, 128, 8, 64]), ("od_w_out", [2, D, D]),
    ("ffn_w_in", [4, D, 2 * DFF]), ("ffn_conv_w", [4, 3, 2 * DFF]), ("ffn_conv_b", [4, 2 * DFF]),
    ("ffn_w_out", [4, DFF, D]),
]
NCP = 12


def make_cpack():
    p = np.arange(128)[:, None]
    c = np.arange(128)[None, :]
    blocks = [
        (p == c).astype(np.float32),
        np.where(p > c, NEG, 0.0).astype(np.float32),
        np.where(p >= c, NEG, 0.0).astype(np.float32),
        np.where(c > p, -1e30, 0.0).astype(np.float32),
        (p <= c).astype(np.float32),
        np.where(p >= c, -1.0, 0.0).astype(np.float32),
        np.ones((128, 128), np.float32),
        -np.ones((128, 128), np.float32),
        ((c == p + 64) & (p < 16)).astype(np.float32),
        ((c == p) & (p >= 64) & (p < 80)).astype(np.float32),
        (p < c).astype(np.float32),
        np.broadcast_to(2.0 ** (-c.astype(np.float64)), (128, 128)).astype(np.float32),
    ]
    return np.ascontiguousarray(np.concatenate(blocks, axis=1))


class KB:
    def __init__(self, nseq=2, layers=(0, 1, 2, 3), final=True):
        self.nseq = nseq
        self.layers = tuple(layers)
        self.final = final
        self.nc = nc = bass.Bass("TRN2", target_bir_lowering=False)
        self.st = st = ExitStack()
        dt = nc.dram_tensor
        self.x_d = dt("x", [nseq, S, D], F32, kind="ExternalInput").ap()
        self.pos_d = dt("positions", [nseq, S], I32, kind="ExternalInput").ap()
        self.w = {}
        for name, shp in WEIGHT_SPECS:
            self.w[name] = dt(name, shp, F32, kind="ExternalInput").ap()
        self.cpk_d = dt("cpack", [128, NCP * 128], F32, kind="ExternalInput").ap()
        self.out_d = dt("out", [nseq, S, D], F32, kind="ExternalOutput").ap()

        sb = lambda n, s, d: st.enter_context(nc.sbuf_tensor(n, s, d))
        self.X = sb("X", [128, NT, D], F32)
        self.HT = sb("HT", [128, 8, S], BF16)
        self.arena = sb("arena", [128, NPAGES * PAGE // 2], BF16)
        self.CF = sb("CF", [128, NCP, 128], F32)
        self.CB = sb("CB", [128, NCP, 128], BF16)
        self.GT = sb("GT", [128, 72], F32)
        self.COS = sb("COS", [128, NT, 8], F32)
        self.SIN = sb("SIN", [128, NT, 8], F32)
        self.SM = sb("SM", [128, 512], F32)
        self.ps = [st.enter_context(nc.psum_tensor("ps%d" % i, [128, 512], F32)) for i in range(8)]
        self.P = Prog(nc, st)
        self.semcache = {}
        self.bx = [Buf("x%d" % t) for t in range(NT)]
        self.bh = [Buf("h%d" % t) for t in range(NT)]
        self.pages = [Buf("pg%d" % i) for i in range(NPAGES)]
        self.bps = [Buf("ps%d" % i, excl=True) for i in range(8)]
        self.bCF, self.bCB, self.bGT, self.bROPE = Buf("CF"), Buf("CB"), Buf("GT"), Buf("rope")
        self.bsm = {}
        self.xsem = [SemH() for _ in range(4)]
        self.smallsem = SemH()
        self.rr = 0
        self.pti = 0
        self.deferred = None
        self.build()

    def smb(self, key):
        if key not in self.bsm:
            self.bsm[key] = Buf("sm_" + key)
        return self.bsm[key]

    def op(self, eng, fn, r=(), w=()):
        self.P.op(eng, fn, flat(r), flat(w))

    def pe(self, fn, r=(), w=()):
        self.op("pe", fn, r, w)

    def act(self, fn, r=(), w=()):
        self.op("act", fn, r, w)

    def dve(self, fn, r=(), w=()):
        self.op("dve", fn, r, w)

    def wload(self, reg, out_ap, in_ap, eng="pool"):
        self.P.dma(eng, lambda e: e.dma_start(out=out_ap, in_=in_ap), reg.semh, reads=(), writes=flat(reg))

    def sload(self, bufs, out_ap, in_ap, semh=None, **kw):
        if semh is None:
            semh = bufs.semh if isinstance(bufs, Reg) else self.smallsem
        self.P.dma("sp", lambda e: e.dma_start(out=out_ap, in_=in_ap, **kw), semh,
                   reads=(), writes=flat(bufs))

    def psb(self, i):
        return self.ps[i][:].bitcast(BF16)

    def mm(self, out, lhsT, rhs, start, stop, r, w, skip=False):
        if skip:
            self.pe(lambda e: e.matmul(out, lhsT=lhsT, rhs=rhs, start=start, stop=stop, skip_group_check=True), r, w)
        else:
            self.pe(lambda e: e.matmul(out, lhsT=lhsT, rhs=rhs, start=start, stop=stop), r, w)

    def build(self):
        nc = self.nc
        self.setup_consts()
        for s in range(self.nseq):
            self.load_x(s)
            self.rope_tables(s)
            for L in self.layers:
                if L % 2 == 0:
                    self.even_mixer(L)
                else:
                    self.odd_mixer(L)
                self.ffn(L)
            self.store_out(s)
        with nc.Block() as block:
            self.P.emit(block)

    def setup_consts(self):
        CF, CB = self.CF, self.CB
        self.sload([self.bCF], CF[:].rearrange("p a b -> p (a b)"), self.cpk_d[:, :])
        sh = SemH()
        self.P.dma("pool", lambda e: e.dma_start(out=CB[:].rearrange("p a b -> p (a b)"), in_=self.cpk_d[:, :]), sh,
                   reads=(), writes=[self.bCB])
        self.identf = CF[:, 0, :]
        self.maskqk = CF[:, 3, :]
        self.tri32 = CF[:, 4, :]
        self.ones32 = CF[:, 6, :]
        self.identb = CB[:, 0, :]
        self.mask_incl = CB[:, 1, :]
        self.mask_strict = CB[:, 2, :]
        self.negU = CB[:, 5, :]
        self.onesb = CB[:, 6, :]
        self.negones = CB[:, 7, :]
        self.sela = CB[:, 8, :]
        self.selb = CB[:, 9, :]
        self.strict01 = CB[:, 10, :]
        self.pow2 = CF[:, 11, :]
        bm = Bump(self, OTOK_PAGES, NPAGES)
        stg = bm.alloc(128 * 4)
        sv = stg.ap(F32)
        self.dve(lambda e: e.memset(sv, 0.0), [], [stg])
        self.sload(stg, sv[0:32, :], self.w["attn_norm"].rearrange("l (c p) -> (l c) p", p=128))
        self.sload(stg, sv[32:64, :], self.w["ffn_norm"].rearrange("l (c p) -> (l c) p", p=128))
        self.sload(stg, sv[64:72, :], self.w["final_norm"].rearrange("(c p) -> c p", p=128))
        self.pe(lambda e: e.transpose(out=self.ps[7][:, 0:128], in_=sv, identity=self.identf),
                [stg, self.bCF], [self.bps[7]])
        self.act(lambda e: e.activation(out=self.GT[:], in_=self.ps[7][:, 0:72], func=AF.Copy),
                 [self.bps[7]], [self.bGT])

    def load_x(self, s):
        xv = self.x_d[s].rearrange("(t p) d -> p t d", p=128)
        for g in range(4):
            self.sload(self.bx[4 * g:4 * g + 4], self.X[:, 4 * g:4 * g + 4, :], xv[:, 4 * g:4 * g + 4, :],
                       semh=self.xsem[g])

    def rope_tables(self, s):
        bm = Bump(self, OTOK_PAGES, NPAGES)
        r = bm.alloc(4096)
        v = r.ap(F32)
        posi = v[:, 0:16].bitcast(I32)
        posf = v[:, 16:32]
        invf = v[:, 32:40]
        v3 = lambda a: v[:, a:a + 128].rearrange("p (t i) -> p t i", t=16)
        tt, u, fl = v3(128), v3(256), v3(512)
        ui = v[:, 384:512].bitcast(I32).rearrange("p (t i) -> p t i", t=16)
        self.sload(r, posi, self.pos_d[s].rearrange("(t p) -> p t", p=128), allow_slow_non_contiguous=True)
        self.dve(lambda e: e.tensor_copy(out=posf, in_=posi), [r], [r])
        for i in range(8):
            val = (500000.0 ** (-(2.0 * i) / 16.0)) / (2.0 * math.pi)
            self.dve(lambda e, i=i, val=val: e.memset(v[:, 32 + i:33 + i], val), [], [r])
        self.dve(lambda e: e.tensor_tensor(out=tt, in0=posf.unsqueeze(2).broadcast_to([128, 16, 8]),
                                           in1=invf.unsqueeze(1).broadcast_to([128, 16, 8]), op=ALU.mult), [r], [r])
        for tab, shift in ((self.SIN, 0.0), (self.COS, 0.25)):
            self.dve(lambda e, shift=shift: e.tensor_scalar(out=u, in0=tt, scalar1=shift, scalar2=None, op0=ALU.add), [r], [r])
            self.dve(lambda e: e.tensor_copy(out=ui, in_=u), [r], [r])
            self.dve(lambda e: e.tensor_copy(out=fl, in_=ui), [r], [r])
            self.dve(lambda e: e.tensor_tensor(out=u, in0=u, in1=fl, op=ALU.subtract), [r], [r])
            self.dve(lambda e: e.tensor_scalar(out=fl, in0=u, scalar1=0.5, scalar2=None, op0=ALU.is_gt), [r], [r])
            self.dve(lambda e: e.tensor_tensor(out=u, in0=u, in1=fl, op=ALU.subtract), [r], [r])
            self.dve(lambda e: e.tensor_scalar(out=fl, in0=u, scalar1=-0.5, scalar2=None, op0=ALU.is_lt), [r], [r])
            self.dve(lambda e: e.tensor_tensor(out=u, in0=u, in1=fl, op=ALU.add), [r], [r])
            self.act(lambda e, tab=tab: e.activation(out=tab[:], in_=u, func=AF.Sin, scale=6.28318), [r], [self.bROPE])

    def rmsnorm_hT(self, gidx, bm):
        X, HT, SM = self.X, self.HT, self.SM
        junk = bm.alloc(2048)
        xn = [bm.alloc(2048), bm.alloc(2048), bm.alloc(2048)]
        bs = self.smb("ss")
        ss = SM[:, 0:16]
        lnv = SM[:, 16:32]
        rstd = SM[:, 32:48]
        for t in range(NT):
            self.act(lambda e, t=t: e.activation(out=junk.ap(), in_=X[:, t, :], func=AF.Square, accum_out=ss[:, t:t + 1]),
                     [self.bx[t]], [junk, bs])
        self.act(lambda e: e.activation(out=lnv, in_=ss, func=AF.Ln, scale=1.0 / D, bias=EPS), [bs], [bs])
        self.act(lambda e: e.activation(out=rstd, in_=lnv, func=AF.Exp, scale=-0.5), [bs], [bs])
        gt = self.GT[:, gidx * 8:(gidx + 1) * 8]

        def scale_tile(t):
            xr = xn[t % 3]
            self.act(lambda e, t=t, xr=xr: e.activation(out=xr.ap(), in_=X[:, t, :], func=AF.Copy, scale=rstd[:, t:t + 1]),
                     [self.bx[t], bs], [xr])
        scale_tile(0)
        for t in range(NT):
            xr = xn[t % 3]
            if t + 1 < NT:
                scale_tile(t + 1)
            bank = 5 + (t % 3)
            pb = self.psb(bank)
            for c in range(8):
                self.pe(lambda e, c=c, xr=xr, pb=pb: e.transpose(out=pb[:, c * 128:(c + 1) * 128],
                                                                in_=xr.ap()[:, c * 128:(c + 1) * 128],
                                                                identity=self.identb),
                        [xr, self.bCB], [self.bps[bank]])
            self.dve(lambda e, t=t, pb=pb: e.tensor_tensor(
                out=HT[:, :, t * 128:(t + 1) * 128],
                in0=pb.rearrange("p (c k) -> p c k", c=8),
                in1=gt.unsqueeze(2).broadcast_to([128, 8, 128]), op=ALU.mult),
                [self.bps[bank], self.bGT], [self.bh[t]])

    def proj_tok(self, wv, wreg, ncols, t, bank, c0=0):
        o = self.ps[bank][:, c0:c0 + ncols]
        for c in range(8):
            self.mm(o, self.HT[:, c, t * 128:(t + 1) * 128], wv[:, c, 0:ncols], c == 0, c == 7,
                    [self.bh[t], wreg], [self.bps[bank]])

    def proj_feat(self, wv, wreg, M, ch, bank):
        o = self.ps[bank][0:M, :]
        for c in range(8):
            self.mm(o, wv[:, c, 0:M], self.HT[:, c, ch * 512:(ch + 1) * 512], c == 0, c == 7,
                    [self.bh[4 * ch:4 * ch + 4], wreg], [self.bps[bank]])

    def flush_deferred(self):
        d = getattr(self, "deferred", None)
        if d:
            self.deferred = None
            d[0]()
            if d[1] is not None:
                d[1]()

    def attn_chunk(self, QT, KT, qreg, kreg, Vfn, vreg, dv1, Ofn, qc, bias_fn, bias_bufs, PT, started, post=None):
        nkt = 4 * qc + 4
        prev = None
        for kt in range(nkt):
            jd = kt - 4 * qc
            c0 = max(0, jd) * 128
            sbk = self.rr % 2
            self.rr += 1
            sb = self.ps[sbk]
            diag = jd >= 0
            self.mm(sb[:, c0:512], KT[:, kt * 128:(kt + 1) * 128],
                    QT[:, qc * 512 + c0:(qc + 1) * 512], True, not diag,
                    [qreg, kreg], [self.bps[sbk]])
            if diag:
                self.mm(sb[:, c0:c0 + 128], self.identb, self.mask_incl, False, True,
                        [self.bCB], [self.bps[sbk]])
            if kt == 0:
                self.flush_deferred()
            if prev is not None:
                self._pv(prev, qc, Vfn, vreg, dv1, Ofn, started)
            pt = PT[self.pti % len(PT)]
            self.pti += 1
            ptv = pt.ap()
            b_ap = bias_fn(kt) if bias_fn is not None else 0.0
            self.act(lambda e, sb=sb, ptv=ptv, c0=c0, b_ap=b_ap: e.activation(
                out=ptv[:, c0:512], in_=sb[:, c0:512], func=AF.Exp, scale=0.125, bias=b_ap),
                [self.bps[sbk], bias_bufs], [pt])
            prev = (kt, pt, c0)
        self.deferred = (lambda: self._pv(prev, qc, Vfn, vreg, dv1, Ofn, started), post)

    def _pv(self, prev, qc, Vfn, vreg, dv1, Ofn, started):
        kt, pt, c0 = prev
        ptv = pt.ap()
        for j in range(c0 // 128, 4):
            o, bank = Ofn(j)
            first = bank not in started
            started.add(bank)
            last = (kt == 4 * qc + j)
            self.mm(o, ptv[:, j * 128:(j + 1) * 128], Vfn(kt), first, last, [pt, vreg], [self.bps[bank]], skip=True)

    def even_mixer(self, L):
        j = L // 2
        X, HT, SM = self.X, self.HT, self.SM
        W = self.w["ev_w_in"][j].rearrange("(c p) n -> p c n", p=128)
        lam_init = 0.8 - 0.6 * math.exp(-0.3 * L)
        bm = Bump(self, OTOK_PAGES, NPAGES)
        self.rmsnorm_hT(L, bm)
        bm = Bump(self, OTOK_PAGES, NPAGES)
        otok = self.arena[:, 0:OTOK_PAGES * PAGE // 2].rearrange("p (h t c) -> p h t c", h=2, t=NT)
        botok = lambda half, t: self.pages[half * 16 + t]

        stop = 99
        if stop <= 1:
            return
        bsp = self.smb("evsmall")
        sp = bm.alloc(4096)
        spv = sp.ap(F32)
        fb = spv[:, 0:8]
        lq = spv[:, 8:264].rearrange("p (a b) -> p a b", a=4)
        subg = spv[:, 264:392]
        lamv = spv[:, 392:400]
        ljunk = spv[:, 400:464]
        self.sload(sp, fb, self.w["ev_fgate_b"][j:j + 1, :].broadcast_to([128, 8]))
        for a, nm in enumerate(("ev_lambda_q1", "ev_lambda_k1", "ev_lambda_q2", "ev_lambda_k2")):
            self.sload(sp, lq[:, a, :], self.w[nm][j:j + 1, :].broadcast_to([128, 64]))
        self.sload(sp, subg, self.w["ev_subln"][j:j + 1, :].broadcast_to([128, 128]))
        for a in range(2):
            self.dve(lambda e, a=a: e.scalar_tensor_tensor(out=ljunk, in0=lq[:, 2 * a, :], scalar=1.0, in1=lq[:, 2 * a + 1, :],
                                                          op0=ALU.mult, op1=ALU.mult, accum_out=lamv[:, a:a + 1]), [sp], [sp])
        self.act(lambda e: e.activation(out=lamv[:, 2:4], in_=lamv[:, 0:2], func=AF.Exp), [sp], [sp])
        self.dve(lambda e: e.tensor_tensor(out=lamv[:, 4:5], in0=lamv[:, 2:3], in1=lamv[:, 3:4], op=ALU.subtract), [sp], [sp])
        self.dve(lambda e: e.tensor_scalar(out=lamv[:, 5:6], in0=lamv[:, 4:5], scalar1=lam_init, scalar2=-1.0,
                                           op0=ALU.add, op1=ALU.mult), [sp], [sp])
        nlam = lamv[:, 5:6]
        self.dve(lambda e: e.tensor_scalar(out=subg, in0=subg, scalar1=1.0 - lam_init, scalar2=None, op0=ALU.mult), [sp], [sp])

        Vreg = bm.alloc(17 * PAGE)
        QT = bm.alloc(4096)
        KTs = [bm.alloc(4096), bm.alloc(4096)]
        PT = [bm.alloc(1024) for _ in range(4)]
        wq = bm.alloc(2048)
        wk = bm.alloc(2048)
        wqk = bm.alloc(4096)
        misc = bm.alloc(8192)
        qkt = [bm.alloc(1024), bm.alloc(1024)]
        wv_reg = Reg(self, QT.off, 8192)
        mv = misc.ap(F32)
        tl = mv[:, 0:128]
        lfp = mv[:, 128:256].rearrange("p (t h) -> p t h", t=16)
        cp = mv[:, 256:384].rearrange("p (t h) -> p t h", t=16)
        cref = mv[:, 384:416].rearrange("p (q h) -> p q h", q=4)
        biasF = mv[:, 512:1024].rearrange("p (q t h) -> p q t h", q=4, t=16)
        rec = mv[:, 1024:1040]
        tmpD = mv[:, 1152:1664].rearrange("p (j c) -> p j c", j=4)
        tmpA = mv[:, 1664:1792]
        ssd = mv[:, 1040:1044]
        lnd = mv[:, 1044:1048]
        rsd = mv[:, 1048:1052]
        djunk = mv[:, 1792:1920]
        ropet = mv[:, 1920:2048].rearrange("p (a m i) -> p a m i", a=4, m=4)

        wvv = wv_reg.ap().rearrange("p (c n) -> p c n", c=8)
        self.wload(wv_reg, wvv, W[:, :, 1024:1536])
        waf = wq
        wafv = waf.ap()[:, 0:64].rearrange("p (c n) -> p c n", c=8)
        self.wload(waf, wafv, W[:, :, 1536:1544])
        Vv = Vreg.ap()[:, 0:16 * 8 * 66].rearrange("p (t h d) -> p t h d", t=16, h=8)
        self.dve(lambda e: e.memset(Vv[:, :, :, 64:66], 1.0), [], [Vreg])
        for t in range(NT):
            bank = 5 + (t % 2)
            self.proj_tok(wvv, wv_reg, 512, t, bank)
            self.act(lambda e, t=t, bank=bank: e.activation(out=Vv[:, t, :, 0:64],
                                                           in_=self.ps[bank][:].rearrange("p (h d) -> p h d", h=8),
                                                           func=AF.Copy), [self.bps[bank]], [Vreg])
            o = self.ps[7][:, t * 8:(t + 1) * 8]
            for c in range(8):
                self.mm(o, HT[:, c, t * 128:(t + 1) * 128], wafv[:, c, :], c == 0, c == 7, [self.bh[t], waf], [self.bps[7]])
        self.dve(lambda e: e.tensor_tensor(out=tl.rearrange("p (t h) -> p t h", t=16),
                                           in0=self.ps[7][:, 0:128].rearrange("p (t h) -> p t h", t=16),
                                           in1=fb.unsqueeze(1).broadcast_to([128, 16, 8]), op=ALU.add),
                 [self.bps[7], sp], [misc])
        self.act(lambda e: e.activation(out=tl, in_=tl, func=AF.Exp, scale=-1.0), [misc], [misc])
        self.act(lambda e: e.activation(out=lfp.rearrange("p t h -> p (t h)"), in_=tl, func=AF.Ln, bias=1.0), [misc], [misc])
        for T in range(NT):
            o = self.ps[7][:, T * 8:(T + 1) * 8]
            for t in range(T + 1):
                self.mm(o, self.tri32 if t == T else self.ones32, lfp[:, t, :], t == 0, t == T, [misc, self.bCF], [self.bps[7]])
        self.act(lambda e: e.activation(out=cp.rearrange("p t h -> p (t h)"), in_=self.ps[7][:, 0:128], func=AF.Copy),
                 [self.bps[7]], [misc])
        for qc in range(4):
            o = self.ps[6][:, qc * 8:(qc + 1) * 8]
            n = 4 * qc + 2
            for t in range(n):
                self.mm(o, self.ones32, lfp[:, t, :], t == 0, t == n - 1, [misc, self.bCF], [self.bps[6]])
        self.act(lambda e: e.activation(out=cref.rearrange("p q h -> p (q h)"), in_=self.ps[6][:, 0:32], func=AF.Copy),
                 [self.bps[6]], [misc])
        for qc in range(4):
            n = 4 * qc + 4
            self.dve(lambda e, qc=qc, n=n: e.tensor_tensor(out=biasF[:, qc, 0:n, :], in0=cp[:, 0:n, :],
                                                          in1=cref[:, qc, :].unsqueeze(1).broadcast_to([128, n, 8]),
                                                          op=ALU.subtract), [misc], [misc])

        if stop <= 2:
            return
        QTv = QT.ap()
        KTv = [KTs[0].ap(), KTs[1].ap()]
        self.dve(lambda e: e.memset(KTv[0][64:128, :], 0.0), [], [KTs[0]])
        self.dve(lambda e: e.memset(KTv[1][0:64, :], 0.0), [], [KTs[1]])
        wqv = wq.ap().rearrange("p (c n) -> p c n", c=8)
        wkv = wk.ap().rearrange("p (c n) -> p c n", c=8)
        ob = 0
        for hp in range(4):
            self.wload(wq, wqv, W[:, :, hp * 128:(hp + 1) * 128])
            self.wload(wk, wkv, W[:, :, 512 + hp * 128:512 + (hp + 1) * 128])
            for ch in range(4):
                self.proj_feat(wqv, wq, 128, ch, 5)
                self.act(lambda e, ch=ch: e.activation(out=QTv[:, ch * 512:(ch + 1) * 512], in_=self.ps[5][:], func=AF.Copy),
                         [self.bps[5]], [QT.sub(ch * 1024, 1024)])
                self.proj_feat(wkv, wk, 128, ch, 6)
                self.act(lambda e, ch=ch: e.activation(out=KTv[0][0:64, ch * 512:(ch + 1) * 512], in_=self.ps[6][0:64, :],
                                                       func=AF.Copy), [self.bps[6]], [KTs[0].sub(ch * 1024, 1024)])
                self.dve(lambda e, ch=ch: e.tensor_copy(out=KTv[1][64:128, ch * 512:(ch + 1) * 512], in_=self.ps[6][64:128, :]),
                         [self.bps[6]], [KTs[1].sub(ch * 1024, 1024)])
            for hh in range(2):
                h = 2 * hp + hh
                for qc in range(4):
                    bank = 2 + (ob % 2)
                    ob += 1
                    Ov = self.ps[bank][:, 0:260].rearrange("p (j d) -> p j d", j=4)
                    started = set()
                    def post(Ov=Ov, qc=qc, h=h, bank=bank):
                        self.dve(lambda e: e.reciprocal(out=rec[:, 0:4], in_=Ov[:, :, 64]), [self.bps[bank]], [misc])
                        self.dve(lambda e: e.tensor_tensor(
                            out=otok[:, 0, 4 * qc:4 * qc + 4, h * 64:(h + 1) * 64], in0=Ov[:, :, 0:64],
                            in1=rec[:, 0:4].unsqueeze(2).broadcast_to([128, 4, 64]), op=ALU.mult),
                            [self.bps[bank], misc], [botok(0, 4 * qc + jj) for jj in range(4)])
                    self.attn_chunk(QTv, KTv[hh], QT.sub(qc * 1024, 1024), KTs[hh],
                                    lambda kt, h=h: Vv[:, kt, h, 0:65], Vreg, 65,
                                    lambda jj, Ov=Ov, bank=bank: (Ov[:, jj, :], bank), qc,
                                    lambda kt, qc=qc, h=h: biasF[:, qc, kt, h:h + 1], misc, PT, started, post=post)

        self.flush_deferred()
        if stop <= 3:
            return
        Vd = Vreg.ap()[:, 0:16 * 4 * 130].rearrange("p (t h d) -> p t h d", t=16, h=4)
        self.wload(wv_reg, wvv, W[:, :, 2568:3080])
        self.dve(lambda e: e.memset(Vd[:, :, :, 128:130], 1.0), [], [Vreg])
        for t in range(NT):
            bank = 5 + (t % 2)
            self.proj_tok(wvv, wv_reg, 512, t, bank)
            self.act(lambda e, t=t, bank=bank: e.activation(out=Vd[:, t, :, 0:128],
                                                           in_=self.ps[bank][:].rearrange("p (h d) -> p h d", h=4),
                                                           func=AF.Copy), [self.bps[bank]], [Vreg])
        self.dve(lambda e: e.memset(KTv[0][64:128, :], 0.0), [], [KTs[0]])
        self.dve(lambda e: e.memset(KTv[1][0:64, :], 0.0), [], [KTs[1]])
        wqkv = wqk.ap().rearrange("p (c n) -> p c n", c=8)
        Oacc0 = lambda a, base: (self.ps[base + a // 3][:, (a % 3) * 129:(a % 3 + 1) * 129], base + a // 3)
        rnd = 0
        for h in range(4):
            self.flush_deferred()
            self.wload(wqk, wqkv[:, :, 0:128], W[:, :, 1544 + h * 128:1544 + (h + 1) * 128])
            self.wload(wqk, wqkv[:, :, 128:256], W[:, :, 2056 + h * 128:2056 + (h + 1) * 128])
            self.proj_tok(wqkv, wqk, 256, 0, 5)
            for t in range(NT):
                bank = 5 + (t % 2)
                if t + 1 < NT:
                    self.proj_tok(wqkv, wqk, 256, t + 1, 5 + ((t + 1) % 2))
                qk = qkt[t % 2]
                qkv = qk.ap()[:, 0:256]
                self.act(lambda e, bank=bank, qkv=qkv: e.activation(out=qkv, in_=self.ps[bank][:, 0:256], func=AF.Copy),
                         [self.bps[bank]], [qk])
                self.rope_tok(self.ps[bank][:, 0:256].rearrange("p (m d) -> p m d", m=4), self.bps[bank],
                              qkv.rearrange("p (m d) -> p m d", m=4), qk, 4, t, ropet, misc)
                pb = self.psb(7)
                for a in range(2):
                    self.pe(lambda e, a=a, qkv=qkv, pb=pb: e.transpose(out=pb[:, a * 128:(a + 1) * 128],
                                                                      in_=qkv[:, a * 128:(a + 1) * 128], identity=self.identb),
                            [qk, self.bCB], [self.bps[7]])
                self.act(lambda e, t=t, pb=pb: e.activation(out=QTv[:, t * 128:(t + 1) * 128], in_=pb[:, 0:128], func=AF.Copy),
                         [self.bps[7]], [QT.sub((t // 4) * 1024, 1024)])
                self.act(lambda e, t=t, pb=pb: e.activation(out=KTv[0][0:64, t * 128:(t + 1) * 128], in_=pb[0:64, 128:256],
                                                           func=AF.Copy), [self.bps[7]], [KTs[0].sub((t // 4) * 1024, 1024)])
                self.act(lambda e, t=t, pb=pb: e.activation(out=KTv[1][64:128, t * 128:(t + 1) * 128], in_=pb[64:128, 128:256],
                                                           func=AF.Copy), [self.bps[7]], [KTs[1].sub((t // 4) * 1024, 1024)])
            for qc in range(4):
                started = set()
                obase = 2 if rnd % 2 == 0 else 5
                rnd += 1
                Oacc = lambda a, obase=obase: Oacc0(a, obase)

                def post(qc=qc, h=h, Oacc=Oacc, obase=obase):
                    obufs = [self.bps[obase], self.bps[obase + 1], self.bps[obase + 2]]
                    for jj in range(4):
                        a1, _ = Oacc(jj)
                        a2, _ = Oacc(4 + jj)
                        self.dve(lambda e, a1=a1, jj=jj: e.reciprocal(out=rec[:, jj:jj + 1], in_=a1[:, 128:129]), obufs, [misc])
                        self.dve(lambda e, a2=a2, jj=jj: e.reciprocal(out=rec[:, 4 + jj:5 + jj], in_=a2[:, 128:129]), obufs, [misc])
                    self.dve(lambda e: e.tensor_scalar(out=rec[:, 4:8], in0=rec[:, 4:8], scalar1=nlam, scalar2=None, op0=ALU.mult),
                             [misc, sp], [misc])
                    for jj in range(4):
                        a1, _ = Oacc(jj)
                        a2, _ = Oacc(4 + jj)
                        self.dve(lambda e, a1=a1, jj=jj: e.tensor_scalar(out=tmpA, in0=a1[:, 0:128], scalar1=rec[:, jj:jj + 1],
                                                                        scalar2=None, op0=ALU.mult), obufs + [misc], [misc])
                        self.dve(lambda e, a2=a2, jj=jj: e.scalar_tensor_tensor(out=tmpD[:, jj, :], in0=a2[:, 0:128],
                                                                               scalar=rec[:, 4 + jj:5 + jj], in1=tmpA,
                                                                               op0=ALU.mult, op1=ALU.add), obufs + [misc], [misc])
                        self.dve(lambda e, jj=jj: e.scalar_tensor_tensor(out=djunk, in0=tmpD[:, jj, :], scalar=1.0, in1=tmpD[:, jj, :],
                                                                        op0=ALU.mult, op1=ALU.mult, accum_out=ssd[:, jj:jj + 1]),
                                 [misc], [misc])
                    self.act(lambda e: e.activation(out=lnd, in_=ssd, func=AF.Ln, scale=1.0 / 128, bias=EPS), [misc], [misc])
                    self.act(lambda e: e.activation(out=rsd, in_=lnd, func=AF.Exp, scale=-0.5), [misc], [misc])
                    for jj in range(4):
                        self.dve(lambda e, jj=jj, qc=qc, h=h: e.scalar_tensor_tensor(
                            out=otok[:, 1, 4 * qc + jj, h * 128:(h + 1) * 128], in0=tmpD[:, jj, :], scalar=rsd[:, jj:jj + 1],
                            in1=subg, op0=ALU.mult, op1=ALU.mult), [misc, sp], [botok(1, 4 * qc + jj)])


                for m in range(2):
                    self.attn_chunk(QTv, KTv[m], QT.sub(qc * 1024, 1024), KTs[m],
                                    lambda kt, h=h: Vd[:, kt, h, 0:129], Vreg, 129,
                                    lambda jj, m=m, Oacc=Oacc: Oacc(4 * m + jj), qc, None, None, PT, started,
                                    post=(post if m == 1 else None))
        self.flush_deferred()
        if stop <= 4:
            return
        self.out_proj(self.w["ev_w_out"][j], Vreg, otok, botok)

    def rope_tok(self, src, bsrc, dst, dreg, nm, t, tmp, treg):
        cosb = self.COS[:, t, :].unsqueeze(1).broadcast_to([128, nm, 8])
        sinb = self.SIN[:, t, :].unsqueeze(1).broadcast_to([128, nm, 8])
        x1, x2 = src[:, :, 0:8], src[:, :, 8:16]
        t1, t2, t3, t4 = tmp[:, 0, 0:nm, :], tmp[:, 1, 0:nm, :], tmp[:, 2, 0:nm, :], tmp[:, 3, 0:nm, :]
        rd = [bsrc, self.bROPE]
        self.dve(lambda e: e.tensor_tensor(out=t1, in0=x1, in1=cosb, op=ALU.mult), rd, [treg])
        self.dve(lambda e: e.tensor_tensor(out=t2, in0=x2, in1=sinb, op=ALU.mult), rd, [treg])
        self.dve(lambda e: e.tensor_tensor(out=t3, in0=x2, in1=cosb, op=ALU.mult), rd, [treg])
        self.dve(lambda e: e.tensor_tensor(out=t4, in0=x1, in1=sinb, op=ALU.mult), rd, [treg])
        self.dve(lambda e: e.tensor_tensor(out=dst[:, :, 0:8], in0=t1, in1=t2, op=ALU.subtract), [treg], [dreg])
        self.dve(lambda e: e.tensor_tensor(out=dst[:, :, 8:16], in0=t3, in1=t4, op=ALU.add), [treg], [dreg])

    def out_proj(self, wo_d, woreg, otok, botok):
        HT, X = self.HT, self.X
        wo = Reg(self, woreg.off, 16384)
        wov = wo.ap().rearrange("p (c n) -> p c n", c=8)
        wd = wo_d.rearrange("(c p) n -> p c n", p=128)
        for c in range(8):
            self.wload(wo, wov[:, c, :], wd[:, c, :])
        for t in range(NT):
            bank = 5 + (t % 3)
            pb = self.psb(bank)
            for c in range(8):
                self.pe(lambda e, c=c, t=t, pb=pb: e.transpose(out=pb[:, c * 128:(c + 1) * 128],
                                                              in_=otok[:, c // 4, t, (c % 4) * 128:(c % 4 + 1) * 128],
                                                              identity=self.identb),
                        [botok(c // 4, t), self.bCB], [self.bps[bank]])
            self.act(lambda e, t=t, pb=pb: e.activation(out=HT[:, :, t * 128:(t + 1) * 128],
                                                       in_=pb.rearrange("p (c k) -> p c k", c=8), func=AF.Copy),
                     [self.bps[bank]], [self.bh[t]])
        for t in range(NT):
            for half in range(2):
                bank = 2 + ((2 * t + half) % 3)
                for c in range(8):
                    self.mm(self.ps[bank][:], HT[:, c, t * 128:(t + 1) * 128], wov[:, c, half * 512:(half + 1) * 512],
                            c == 0, c == 7, [self.bh[t], wo], [self.bps[bank]])
                self.dve(lambda e, t=t, half=half, bank=bank: e.tensor_tensor(
                    out=X[:, t, half * 512:(half + 1) * 512], in0=self.ps[bank][:], in1=X[:, t, half * 512:(half + 1) * 512],
                    op=ALU.add), [self.bps[bank], self.bx[t]], [self.bx[t]])

    def odd_mixer(self, L):
        j = L // 2
        X, HT, SM = self.X, self.HT, self.SM
        W = self.w["od_w_in"][j].rearrange("(c p) n -> p c n", p=128)
        bm = Bump(self, OTOK_PAGES, NPAGES)
        self.rmsnorm_hT(L, bm)
        otok = self.arena[:, 0:OTOK_PAGES * PAGE // 2].rearrange("p (h t c) -> p h t c", h=2, t=NT)
        botok = lambda half, t: self.pages[half * 16 + t]
        stop = 99
        if stop <= 1:
            return
        self.dsa(L, W, otok, botok)
        if stop <= 2:
            return
        vreg = self.sb_attn(L, W, otok, botok)
        if stop <= 3:
            return
        self.out_proj(self.w["od_w_out"][j], vreg, otok, botok)

    def dsa(self, L, W, otok, botok):
        j = L // 2
        HT = self.HT
        bm = Bump(self, 16, NPAGES)
        Vc = bm.alloc(4096)
        KcT = bm.alloc(4096)
        KXT = bm.alloc(4096)
        wQ = bm.alloc(13 * PAGE)
        sc = bm.alloc(8192)
        rt = [bm.alloc(2048), bm.alloc(2048)]
        mneg = bm.alloc(4096)
        PT = [bm.alloc(1024) for _ in range(4)]
        Lrec = [bm.alloc(2048), bm.alloc(2048)]
        OTn = [bm.alloc(1024), bm.alloc(1024)]
        cqtok = bm.alloc(1024)
        iqpad = bm.alloc(1024)
        kxtok = bm.alloc(1024)
        cqT = bm.alloc(1024)
        iqT = bm.alloc(1024)
        QabsT = bm.alloc(2048)
        QrTz = bm.alloc(2048)
        UK = bm.alloc(2048)
        UV = bm.alloc(1024)
        I4 = bm.alloc(1024)
        sml = bm.alloc(1024)
        kvg = bm.alloc(1024)
        rtmp = bm.alloc(1024)
        junk = rt[0]
        wK = Reg(self, Lrec[0].off, 4 * PAGE)
        ukst = bm.alloc(1024)

        smv = sml.ap(F32)
        absw, sgn = smv[:, 0:4], smv[:, 4:8]
        hi_, lo0, w0, mid, cntp, atmp, tau = (smv[:, 8 + k:9 + k] for k in range(7))
        kss, kln, krs = smv[:, 16:17], smv[:, 17:18], smv[:, 18:19]
        wtab = smv[:, 32:32 + NBIS + 2]
        ropet = rtmp.ap(F32)[:, 0:256].rearrange("p (a m i) -> p a m i", a=4, m=8)
        kvgv = kvg.ap(F32)[:, 0:128]
        scv = sc.ap(F32)
        mnegv = mneg.ap()
        Vcv = Vc.ap().rearrange("p (t r) -> p t r", t=16)
        KcTv, KXTv = KcT.ap(), KXT.ap()
        I4v = I4.ap()
        UKv = UK.ap().rearrange("p (h r) -> p h r", h=8)
        UVv = UV.ap().rearrange("p (h d) -> p h d", h=8)

        for q4 in range(4):
            self.dve(lambda e, q4=q4: e.tensor_copy(out=I4v[:, q4 * 128:(q4 + 1) * 128], in_=self.identb), [self.bCB], [I4])
        self.sload(kvg, kvgv, self.w["od_kv_norm"][j:j + 1, :].broadcast_to([128, 128]))
        self.wload(UV, UV.ap(), self.w["od_w_uv"][j].rearrange("r h d -> r (h d)"))
        ukv = ukst.ap().rearrange("p (q c) -> p q c", q=4)
        self.dve(lambda e: e.memset(ukst.ap(), 0.0), [], [ukst])
        self.dve(lambda e: e.memset(UK.ap(), 0.0), [], [UK])
        wuk = self.w["od_w_uk"][j]
        for h in range(8):
            cb = (h % 2) * 64 + 16
            self.wload(ukst, ukv[:, h // 2, cb:cb + 48], wuk[:, h, :])
        pb = self.psb(7)
        for q4 in range(4):
            self.pe(lambda e, q4=q4: e.transpose(out=pb[:, q4 * 128:(q4 + 1) * 128], in_=ukv[:, q4, :], identity=self.identb),
                    [ukst, self.bCB], [self.bps[7]])
        for q4 in range(4):
            self.act(lambda e, q4=q4: e.activation(out=UKv[0:64, 2 * q4, :], in_=pb[0:64, q4 * 128:(q4 + 1) * 128], func=AF.Copy),
                     [self.bps[7]], [UK])
            self.act(lambda e, q4=q4: e.activation(out=UKv[64:128, 2 * q4 + 1, :], in_=pb[64:128, q4 * 128:(q4 + 1) * 128],
                                                   func=AF.Copy), [self.bps[7]], [UK])

        wKv = wK.ap()[:, 0:8 * 208].rearrange("p (c n) -> p c n", c=8)
        self.wload(wK, wKv[:, :, 0:128], W[:, :, 512:640])
        self.wload(wK, wKv[:, :, 128:192], W[:, :, 916:980])
        self.wload(wK, wKv[:, :, 192:208], W[:, :, 640:656])
        kxv = kxtok.ap()[:, 0:128]
        self.dve(lambda e: e.memset(kxv, 0.0), [], [kxtok])
        self.proj_tok(wKv, wK, 208, 0, 6)
        for t in range(NT):
            bank = 6 + (t % 2)
            pbk = self.ps[bank]
            if t + 1 < NT:
                self.proj_tok(wKv, wK, 208, t + 1, 6 + ((t + 1) % 2))
            self.act(lambda e, pbk=pbk: e.activation(out=junk.ap(F32)[:, 0:128], in_=pbk[:, 0:128], func=AF.Square, accum_out=kss),
                     [self.bps[bank]], [junk, sml])
            self.act(lambda e: e.activation(out=kln, in_=kss, func=AF.Ln, scale=1.0 / 128, bias=EPS), [sml], [sml])
            self.act(lambda e: e.activation(out=krs, in_=kln, func=AF.Exp, scale=-0.5), [sml], [sml])
            self.dve(lambda e, t=t, pbk=pbk: e.scalar_tensor_tensor(out=Vcv[:, t, :], in0=pbk[:, 0:128], scalar=krs, in1=kvgv,
                                                                   op0=ALU.mult, op1=ALU.mult),
                     [self.bps[bank], sml, kvg], [Vc])
            self.act(lambda e, pbk=pbk: e.activation(out=kxv[:, 0:64], in_=pbk[:, 128:192], func=AF.Copy), [self.bps[bank]], [kxtok])
            self.act(lambda e, pbk=pbk: e.activation(out=kxv[:, 64:80], in_=pbk[:, 192:208], func=AF.Copy), [self.bps[bank]], [kxtok])
            self.rope_tok(pbk[:, 128:256].rearrange("p (m d) -> p m d", m=2), self.bps[bank],
                          kxv.rearrange("p (m d) -> p m d", m=2), kxtok, 2, t, ropet, rtmp)
            pb5 = self.psb(5)
            self.pe(lambda e, t=t, pb5=pb5: e.transpose(out=pb5[:, 0:128], in_=Vcv[:, t, :], identity=self.identb),
                    [Vc, self.bCB], [self.bps[5]])
            self.pe(lambda e, pb5=pb5: e.transpose(out=pb5[:, 128:256], in_=kxv, identity=self.identb),
                    [kxtok, self.bCB], [self.bps[5]])
            self.act(lambda e, t=t, pb5=pb5: e.activation(out=KcTv[:, t * 128:(t + 1) * 128], in_=pb5[:, 0:128], func=AF.Copy),
                     [self.bps[5]], [KcT.sub((t // 4) * 1024, 1024)])
            self.act(lambda e, t=t, pb5=pb5: e.activation(out=KXTv[:, t * 128:(t + 1) * 128], in_=pb5[:, 128:256], func=AF.Copy),
                     [self.bps[5]], [KXT.sub((t // 4) * 1024, 1024)])

        wQv = wQ.ap()[:, 0:8 * 772].rearrange("p (c n) -> p c n", c=8)
        self.wload(wQ, wQv[:, :, 0:512], W[:, :, 0:512])
        self.wload(wQ, wQv[:, :, 512:772], W[:, :, 656:916])
        cqv = cqtok.ap()
        iqv = iqpad.ap().rearrange("p (h c) -> p h c", h=4)
        self.dve(lambda e: e.memset(iqpad.ap(), 0.0), [], [iqpad])
        cqTv = cqT.ap().rearrange("p (q k) -> p q k", q=4)
        iqTv = iqT.ap().rearrange("p (q k) -> p q k", q=4)
        QabsS = [QabsT, bm.alloc(2048)]
        QrTzS = [QrTz, bm.alloc(2048)]
        mnegS = [mneg, bm.alloc(4096)]
        junk2 = Reg(self, rt[0].off, 4096)
        assert rt[1].off == rt[0].off + 2048
        junkv = junk2.ap()
        junkr = [rt[0], rt[1]]

        def prep(i, slot):
            nk = (i + 1) * 128
            QabsTv, QrTzv, mnegv = QabsS[slot].ap(), QrTzS[slot].ap(), mnegS[slot].ap()
            self.proj_tok(wQv, wQ, 512, i, 6)
            self.act(lambda e: e.activation(out=cqv, in_=self.ps[6][:], func=AF.Copy), [self.bps[6]], [cqtok])
            self.rope_tok(self.ps[6][:].rearrange("p (m d) -> p m d", m=8), self.bps[6],
                          cqv.rearrange("p (m d) -> p m d", m=8), cqtok, 8, i, ropet, rtmp)
            yield
            o7 = self.ps[7][:, 0:260]
            for c in range(8):
                self.mm(o7, HT[:, c, i * 128:(i + 1) * 128], wQv[:, c, 512:772], c == 0, c == 7, [self.bh[i], wQ], [self.bps[7]])
            self.act(lambda e: e.activation(out=iqv[:, :, 0:64], in_=self.ps[7][:, 0:256].rearrange("p (h d) -> p h d", h=4),
                                            func=AF.Copy), [self.bps[7]], [iqpad])
            self.rope_tok(self.ps[7][:, 0:256].rearrange("p (m d) -> p m d", m=4), self.bps[7],
                          iqv[:, :, 0:64], iqpad, 4, i, ropet, rtmp)
            self.act(lambda e: e.activation(out=absw, in_=self.ps[7][:, 256:260], func=AF.Abs, scale=0.5), [self.bps[7]], [sml])
            self.dve(lambda e: e.tensor_scalar(out=sgn, in0=self.ps[7][:, 256:260], scalar1=0.0, scalar2=2.0,
                                               op0=ALU.is_ge, op1=ALU.mult), [self.bps[7]], [sml])
            self.dve(lambda e: e.tensor_scalar(out=sgn, in0=sgn, scalar1=-1.0, scalar2=None, op0=ALU.add), [sml], [sml])
            yield
            pb6 = self.psb(6)
            for q4 in range(4):
                self.pe(lambda e, q4=q4: e.transpose(out=pb6[:, q4 * 128:(q4 + 1) * 128], in_=cqv[:, q4 * 128:(q4 + 1) * 128],
                                                     identity=self.identb), [cqtok, self.bCB], [self.bps[6]])
            self.act(lambda e: e.activation(out=cqT.ap(), in_=pb6[:, 0:512], func=AF.Copy), [self.bps[6]], [cqT])
            pb7 = self.psb(7)
            for q4 in range(4):
                self.pe(lambda e, q4=q4: e.transpose(out=pb7[:, q4 * 128:(q4 + 1) * 128], in_=iqv[:, q4, :], identity=self.identb),
                        [iqpad, self.bCB], [self.bps[7]])
            self.act(lambda e: e.activation(out=iqT.ap(), in_=pb7[:, 0:512], func=AF.Copy), [self.bps[7]], [iqT])
            yield
            for half in range(2):
                bank = 6 + half
                for hh in range(4):
                    h = 4 * half + hh
                    self.mm(self.ps[bank][:, hh * 128:(hh + 1) * 128], UKv[:, h, :], cqTv[:, h // 2, :], True, True,
                            [UK, cqT], [self.bps[bank]])
                self.act(lambda e, half=half, bank=bank: e.activation(out=QabsTv[:, half * 512:(half + 1) * 512],
                                                                     in_=self.ps[bank][:], func=AF.Copy),
                         [self.bps[bank]], [QabsS[slot].sub(half * 1024, 1024)])
            yield
            for half in range(2):
                bank = 6 + half
                for hh in range(4):
                    h = 4 * half + hh
                    self.mm(self.ps[bank][:, hh * 128:(hh + 1) * 128], self.sela if h % 2 == 0 else self.selb,
                            cqTv[:, h // 2, :], True, True, [self.bCB, cqT], [self.bps[bank]])
                self.act(lambda e, half=half, bank=bank: e.activation(out=QrTzv[:, half * 512:(half + 1) * 512], in_=self.ps[bank][:],
                                                                     func=AF.Copy),
                         [self.bps[bank]], [QrTzS[slot].sub(half * 1024, 1024)])
            yield
            nkc = (nk + 511) // 512
            for kc in range(nkc):
                ncol = min(512, nk - kc * 512)
                for hi in range(4):
                    bank = 6 + ((kc * 4 + hi) % 2)
                    self.mm(self.ps[bank][:, 0:ncol], iqTv[:, hi, :], KXTv[:, kc * 512:kc * 512 + ncol], True, True,
                            [iqT, KXT], [self.bps[bank]])
                    rv = self.ps[bank]
                    self.act(lambda e, rv=rv, ncol=ncol, hi=hi: e.activation(
                        out=rv[:, 0:ncol], in_=rv[:, 0:ncol], func=AF.Relu, scale=absw[:, hi:hi + 1]),
                        [self.bps[bank], sml], [self.bps[bank]])
                    scs = scv[:, kc * 512:kc * 512 + ncol]
                    if hi == 0:
                        self.dve(lambda e, rv=rv, ncol=ncol, scs=scs: e.tensor_scalar(out=scs, in0=rv[:, 0:ncol], scalar1=sgn[:, 0:1],
                                                                                   scalar2=None, op0=ALU.mult), [self.bps[bank], sml], [sc])
                    else:
                        self.dve(lambda e, rv=rv, ncol=ncol, scs=scs, hi=hi: e.scalar_tensor_tensor(
                            out=scs, in0=rv[:, 0:ncol], scalar=sgn[:, hi:hi + 1], in1=scs, op0=ALU.mult, op1=ALU.add),
                            [self.bps[bank], sml, sc], [sc])
                    yield
            self.dve(lambda e, i=i: e.tensor_tensor(out=scv[:, i * 128:(i + 1) * 128], in0=scv[:, i * 128:(i + 1) * 128],
                                                    in1=self.maskqk, op=ALU.add), [sc, self.bCF], [sc])
            if i < 2:
                self.dve(lambda e: e.memset(tau, -1e29), [], [sml])
            else:
                self.dve(lambda e, nk=nk: e.tensor_reduce(out=hi_, in_=scv[:, 0:nk], axis=AX.X, op=ALU.max), [sc], [sml])
                self.dve(lambda e, i=i: e.tensor_reduce(out=lo0, in_=scv[:, 0:i * 128], axis=AX.X, op=ALU.min), [sc], [sml])
                self.dve(lambda e: e.tensor_tensor(out=w0, in0=hi_, in1=lo0, op=ALU.subtract), [sml], [sml])
                self.dve(lambda e: e.tensor_scalar(out=wtab, in0=self.pow2[:, 0:NBIS + 2], scalar1=w0, scalar2=None, op0=ALU.mult),
                         [sml, self.bCF], [sml])
                self.dve(lambda e: e.tensor_tensor(out=mid, in0=lo0, in1=wtab[:, 1:2], op=ALU.add), [sml], [sml])
                for t in range(1, NBIS + 1):
                    self.dve(lambda e, nk=nk: e.tensor_scalar(out=junkv[:, 0:nk], in0=scv[:, 0:nk], scalar1=mid, scalar2=-float(TOPK),
                                                              op0=ALU.is_ge, op1=ALU.add, accum_out=cntp), [sc, sml], [junkr, sml])
                    self.dve(lambda e, t=t: e.scalar_tensor_tensor(out=atmp, in0=cntp, scalar=0.0, in1=wtab[:, t:t + 1],
                                                                   op0=ALU.is_ge, op1=ALU.mult), [sml], [sml])
                    self.dve(lambda e, t=t: e.scalar_tensor_tensor(out=mid, in0=mid, scalar=wtab[:, t + 1:t + 2], in1=atmp,
                                                                   op0=ALU.subtract, op1=ALU.add), [sml], [sml])
                self.dve(lambda e: e.tensor_tensor(out=tau, in0=mid, in1=wtab[:, NBIS + 1:NBIS + 2], op=ALU.subtract), [sml], [sml])
            self.dve(lambda e, nk=nk: e.tensor_scalar(out=mnegv[:, 0:nk], in0=scv[:, 0:nk], scalar1=tau, scalar2=NEG,
                                                      op0=ALU.is_lt, op1=ALU.mult), [sc, sml], [mnegS[slot]])

        def attend(i, slot, gen):
            QabsTv, QrTzv, mnegv = QabsS[slot].ap(), QrTzS[slot].ap(), mnegS[slot].ap()
            pend = []

            def flush(item):
                kt, half, pt = item
                ptv = pt.ap()
                self.mm(self.ps[2 + half][:], Vcv[:, kt, :], ptv, kt == 0, kt == i, [Vc, pt], [self.bps[2 + half]])
                self.mm(self.ps[4 + half][:], self.onesb, ptv, kt == 0, kt == i, [self.bCB, pt], [self.bps[4 + half]])
            pi = 0
            for kt in range(i + 1):
                for half in range(2):
                    sbk = self.ps[half]
                    self.mm(sbk[:], KcTv[:, kt * 128:(kt + 1) * 128], QabsTv[:, half * 512:(half + 1) * 512], True, False,
                            [KcT, QabsS[slot]], [self.bps[half]])
                    self.mm(sbk[:], KXTv[:, kt * 128:(kt + 1) * 128], QrTzv[:, half * 512:(half + 1) * 512], False, False,
                            [KXT, QrTzS[slot]], [self.bps[half]])
                    self.mm(sbk[:], mnegv[:, kt * 128:(kt + 1) * 128], I4v, False, True, [mnegS[slot], I4], [self.bps[half]])
                    if len(pend) >= 2:
                        flush(pend.pop(0))
                    pt = PT[pi % 4]
                    pi += 1
                    self.act(lambda e, sbk=sbk, pt=pt: e.activation(out=pt.ap(), in_=sbk[:], func=AF.Exp, scale=0.125),
                             [self.bps[half]], [pt])
                    pend.append((kt, half, pt))
                    if gen is not None:
                        next(gen, None)
            while pend:
                flush(pend.pop(0))
            if gen is not None:
                for _ in gen:
                    pass
            for half in range(2):
                lr = Lrec[half].ap(F32)
                self.dve(lambda e, half=half, lr=lr: e.reciprocal(out=lr, in_=self.ps[4 + half][:]), [self.bps[4 + half]], [Lrec[half]])
                self.dve(lambda e, half=half, lr=lr: e.tensor_tensor(out=OTn[half].ap(), in0=self.ps[2 + half][:], in1=lr, op=ALU.mult),
                         [self.bps[2 + half], Lrec[half]], [OTn[half]])
            for h in range(8):
                self.mm(self.ps[5][:, h * 64:(h + 1) * 64], OTn[h // 4].ap()[:, (h % 4) * 128:(h % 4 + 1) * 128], UVv[:, h, :],
                        True, True, [OTn[h // 4], UV], [self.bps[5]])
            self.act(lambda e, i=i: e.activation(out=otok[:, 0, i, :], in_=self.ps[5][:], func=AF.Copy), [self.bps[5]], [botok(0, i)])

        for _ in prep(0, 0):
            pass
        for i in range(NT):
            gen = prep(i + 1, (i + 1) % 2) if i + 1 < NT else None
            attend(i, i % 2, gen)

    def sb_attn(self, L, W, otok, botok):
        HT = self.HT
        bm = Bump(self, OTOK_PAGES, NPAGES)
        Vreg = bm.alloc(16 * PAGE)
        QT = bm.alloc(4096)
        KTs = [bm.alloc(4096), bm.alloc(4096)]
        PT = [bm.alloc(1024) for _ in range(4)]
        Eb = [bm.alloc(2048), bm.alloc(2048)]
        spb = [bm.alloc(1024) for _ in range(3)]
        R32 = bm.alloc(2048)
        Rb = [bm.alloc(1024), bm.alloc(1024)]
        wq = bm.alloc(2048)
        wk = bm.alloc(2048)
        wv_reg = Reg(self, QT.off, 8192)
        wvv = wv_reg.ap().rearrange("p (c n) -> p c n", c=8)
        Vv = Vreg.ap().rearrange("p (t h d) -> p t h d", t=16, h=8)
        self.wload(wv_reg, wvv, W[:, :, 2004:2516])
        for t in range(NT):
            bank = 6 + (t % 2)
            self.proj_tok(wvv, wv_reg, 512, t, bank)
            self.act(lambda e, t=t, bank=bank: e.activation(out=Vv[:, t, :, :], in_=self.ps[bank][:].rearrange("p (h d) -> p h d", h=8),
                                                           func=AF.Copy), [self.bps[bank]], [Vreg])
        QTv = QT.ap()
        KTv = [KTs[0].ap(), KTs[1].ap()]
        self.dve(lambda e: e.memset(KTv[0][64:128, :], 0.0), [], [KTs[0]])
        self.dve(lambda e: e.memset(KTv[1][0:64, :], 0.0), [], [KTs[1]])
        wqv = wq.ap().rearrange("p (c n) -> p c n", c=8)
        wkv = wk.ap().rearrange("p (c n) -> p c n", c=8)
        R32v = R32.ap(F32)
        ob = 0
        for hp in range(4):
            self.wload(wq, wqv, W[:, :, 980 + hp * 128:980 + (hp + 1) * 128])
            self.wload(wk, wkv, W[:, :, 1492 + hp * 128:1492 + (hp + 1) * 128])
            for ch in range(4):
                self.proj_feat(wqv, wq, 128, ch, 6)
                self.act(lambda e, ch=ch: e.activation(out=QTv[:, ch * 512:(ch + 1) * 512], in_=self.ps[6][:], func=AF.Copy, scale=0.125),
                         [self.bps[6]], [QT.sub(ch * 1024, 1024)])
                self.proj_feat(wkv, wk, 128, ch, 7)
                self.act(lambda e, ch=ch: e.activation(out=KTv[0][0:64, ch * 512:(ch + 1) * 512], in_=self.ps[7][0:64, :],
                                                       func=AF.Copy), [self.bps[7]], [KTs[0].sub(ch * 1024, 1024)])
                self.dve(lambda e, ch=ch: e.tensor_copy(out=KTv[1][64:128, ch * 512:(ch + 1) * 512], in_=self.ps[7][64:128, :]),
                         [self.bps[7]], [KTs[1].sub(ch * 1024, 1024)])
            for hh in range(2):
                h = 2 * hp + hh
                KTh, KTr = KTv[hh], KTs[hh]
                for qc in range(4):
                    obank = 4 + (ob % 2)
                    ob += 1
                    Ov = self.ps[obank][:, 0:256].rearrange("p (j d) -> p j d", j=4)
                    qreg = QT.sub(qc * 1024, 1024)
                    self.dve(lambda e: e.memset(R32v, 0.0), [], [R32])
                    started = [False]
                    kts = list(range(4 * qc + 3, -1, -1))
                    st1 = {}
                    st2 = {}
                    ri = [0]

                    def stage1(n, kt):
                        c0 = max(0, kt - 4 * qc) * 128
                        ab = n % 2
                        self.mm(self.ps[ab][:, c0:512], KTh[:, kt * 128:(kt + 1) * 128], QTv[:, qc * 512 + c0:(qc + 1) * 512],
                                True, True, [KTr, qreg], [self.bps[ab]])
                        eb = Eb[n % 2]
                        ev = eb.ap(F32)
                        sp = spb[n % 3]
                        spv = sp.ap()
                        self.act(lambda e, ab=ab, ev=ev, c0=c0: e.activation(out=ev[:, c0:512], in_=self.ps[ab][:, c0:512], func=AF.Exp),
                                 [self.bps[ab]], [eb])
                        self.act(lambda e, ev=ev, spv=spv, c0=c0: e.activation(out=spv[:, c0:512], in_=ev[:, c0:512], func=AF.Ln, bias=1.0),
                                 [eb], [sp])
                        if kt >= 4 * qc:
                            self.dve(lambda e, spv=spv, c0=c0: e.tensor_tensor(out=spv[:, c0:c0 + 128], in0=spv[:, c0:c0 + 128],
                                                                              in1=self.strict01, op=ALU.mult), [sp, self.bCB], [sp])
                        st1[n] = (kt, c0, sp, spv)

                    def stage2(n):
                        kt, c0, sp, spv = st1.pop(n)
                        bb = n % 2
                        pbk = self.ps[bb]
                        first = (n == 0)
                        diag = kt >= 4 * qc
                        rbr = Rb[ri[0] % 2]
                        if not first:
                            self.mm(pbk[:, c0:512], self.negones, rbr.ap()[:, c0:512], False, False, [self.bCB, rbr], [self.bps[bb]],
                                    skip=True)
                        self.mm(pbk[:, c0:512], self.negU, spv[:, c0:512], False, not diag, [self.bCB, sp], [self.bps[bb]], skip=True)
                        if diag:
                            self.mm(pbk[:, c0:c0 + 128], self.identb, self.mask_strict, False, True, [self.bCB], [self.bps[bb]],
                                    skip=True)
                        pt = PT[n % 4]
                        self.act(lambda e, pbk=pbk, pt=pt, c0=c0: e.activation(out=pt.ap()[:, c0:512], in_=pbk[:, c0:512], func=AF.Exp),
                                 [self.bps[bb]], [pt])
                        self.dve(lambda e, spv=spv, c0=c0: e.tensor_tensor(out=R32v[:, c0:512], in0=R32v[:, c0:512],
                                                                          in1=spv[:, c0:512], op=ALU.add), [R32, sp], [R32])
                        ri[0] += 1
                        rbn = Rb[ri[0] % 2]
                        self.dve(lambda e, rbn=rbn: e.tensor_copy(out=rbn.ap(), in_=R32v), [R32], [rbn])
                        st2[n] = (kt, c0, pt)

                    def stage3(n):
                        kt, c0, pt = st2.pop(n)
                        for jj in range(c0 // 128, 4):
                            self.mm(Ov[:, jj, :], pt.ap()[:, jj * 128:(jj + 1) * 128], Vv[:, kt, h, :], not started[0], kt == 0,
                                    [pt, Vreg], [self.bps[obank]], skip=True)
                            started[0] = True
                    nn = len(kts)
                    for step in range(nn + 2):
                        if step < nn:
                            stage1(step, kts[step])
                        if 0 <= step - 1 < nn:
                            stage2(step - 1)
                        if 0 <= step - 2 < nn:
                            stage3(step - 2)
                    self.dve(lambda e, Ov=Ov, qc=qc, h=h: e.tensor_copy(out=otok[:, 1, 4 * qc:4 * qc + 4, h * 64:(h + 1) * 64], in_=Ov),
                             [self.bps[obank]], [botok(1, 4 * qc + jj) for jj in range(4)])
        return Vreg

    def ffn(self, L):
        X, HT = self.X, self.HT
        bm = Bump(self, 0, NPAGES)
        self.rmsnorm_hT(4 + L, bm)
        bm = Bump(self, 0, NPAGES)
        stg = bm.alloc(5632 * 4)
        sv = stg.ap(F32)
        self.dve(lambda e: e.memset(sv, 0.0), [], [stg])
        self.sload(stg, sv[0:3, :], self.w["ffn_conv_w"][L])
        self.sload(stg, sv[3:4, :], self.w["ffn_conv_b"][L:L + 1, :])
        cpr = bm.alloc(1024)
        convp = cpr.ap(F32)[:, 0:176].rearrange("p (c k) -> p c k", c=44)
        for g4 in range(11):
            bank = 4 + (g4 % 4)
            for q4 in range(4):
                c = 4 * g4 + q4
                self.pe(lambda e, c=c, q4=q4, bank=bank: e.transpose(out=self.ps[bank][:, q4 * 128:(q4 + 1) * 128],
                                                                    in_=sv[:, c * 128:(c + 1) * 128], identity=self.identf),
                        [stg, self.bCF], [self.bps[bank]])
            self.act(lambda e, g4=g4, bank=bank: e.activation(
                out=convp[:, 4 * g4:4 * g4 + 4, :],
                in_=self.ps[bank][:].rearrange("p (q k) -> p q k", q=4)[:, :, 0:4], func=AF.Copy),
                [self.bps[bank]], [cpr])
        Win = self.w["ffn_w_in"][L].rearrange("(c p) n -> p c n", p=128)
        Wout = self.w["ffn_w_out"][L].rearrange("(c p) n -> p c n", p=128)
        bm2 = bm
        wsl = [(bm2.alloc(4096), bm2.alloc(4096), bm2.alloc(4096)) for _ in range(2)]
        actb = [bm2.alloc(8192) for _ in range(2)]
        ybuf = [bm2.alloc(2048) for _ in range(4)]
        sgb = [bm2.alloc(2048) for _ in range(2)]
        blocks = [(0, 0, 512)]
        t0 = 512
        while t0 < S:
            n = min(510, S - t0)
            blocks.append((t0 - 2, t0, n))
            t0 += n
        NG = NCH // 2
        ub = 0
        db = 0

        def down(G, wo_r, wov, ab):
            nonlocal db
            abv = ab.ap().rearrange("p (c n) -> p c n", c=2)
            for t in range(NT):
                for half in range(2):
                    bank = 4 + (db % 4)
                    db += 1
                    for cc in range(2):
                        self.mm(self.ps[bank][:], abv[:, cc, t * 128:(t + 1) * 128], wov[:, cc, half * 512:(half + 1) * 512],
                                cc == 0, cc == 1, [ab, wo_r], [self.bps[bank]])
                    self.dve(lambda e, t=t, half=half, bank=bank: e.tensor_tensor(
                        out=X[:, t, half * 512:(half + 1) * 512], in0=self.ps[bank][:], in1=X[:, t, half * 512:(half + 1) * 512],
                        op=ALU.add), [self.bps[bank], self.bx[t]], [self.bx[t]])

        pend = None
        for G in range(NG):
            wg_r, wv_r, wo_r = wsl[G % 2]
            ab = actb[G % 2]
            wgv = wg_r.ap().rearrange("p (c n) -> p c n", c=8)
            wvv = wv_r.ap().rearrange("p (c n) -> p c n", c=8)
            wov = wo_r.ap().rearrange("p (c n) -> p c n", c=2)
            self.wload(wg_r, wgv, Win[:, :, G * 256:(G + 1) * 256])
            self.wload(wv_r, wvv, Win[:, :, DFF + G * 256:DFF + (G + 1) * 256])
            self.wload(wo_r, wov, Wout[:, 2 * G:2 * G + 2, :])
            abv = ab.ap().rearrange("p (c n) -> p c n", c=2)
            for (pt0, ot0, n) in blocks:
                ncol = ot0 + n - pt0
                sh = ot0 - pt0
                ys = []
                for s4 in range(4):
                    isv = s4 // 2
                    cc = s4 % 2
                    wsel, wr_ = (wvv, wv_r) if isv else (wgv, wg_r)
                    chn = (22 if isv else 0) + 2 * G + cc
                    bank = ub % 4
                    ub += 1
                    pbk = self.ps[bank]
                    for c in range(8):
                        self.mm(pbk[:, 0:ncol], wsel[:, c, cc * 128:(cc + 1) * 128], HT[:, c, pt0:pt0 + ncol], c == 0, c == 7,
                                [self.bh[pt0 // 128:(pt0 + ncol + 127) // 128], wr_], [self.bps[bank]])
                    y = ybuf[s4]
                    yv = y.ap(F32)
                    w0, w1, w2, bb = (convp[:, chn, k:k + 1] for k in range(4))
                    self.act(lambda e, pbk=pbk, yv=yv, w2=w2, bb=bb, sh=sh, n=n: e.activation(
                        out=yv[:, 0:n], in_=pbk[:, sh:sh + n], func=AF.Identity, scale=w2, bias=bb),
                        [self.bps[bank], cpr], [y])
                    if sh == 0:
                        self.dve(lambda e, pbk=pbk, yv=yv, w1=w1, n=n: e.scalar_tensor_tensor(
                            out=yv[:, 1:n], in0=pbk[:, 0:n - 1], scalar=w1, in1=yv[:, 1:n], op0=ALU.mult, op1=ALU.add),
                            [self.bps[bank], cpr, y], [y])
                        self.dve(lambda e, pbk=pbk, yv=yv, w0=w0, n=n: e.scalar_tensor_tensor(
                            out=yv[:, 2:n], in0=pbk[:, 0:n - 2], scalar=w0, in1=yv[:, 2:n], op0=ALU.mult, op1=ALU.add),
                            [self.bps[bank], cpr, y], [y])
                    else:
                        self.dve(lambda e, pbk=pbk, yv=yv, w1=w1, n=n: e.scalar_tensor_tensor(
                            out=yv[:, 0:n], in0=pbk[:, 1:1 + n], scalar=w1, in1=yv[:, 0:n], op0=ALU.mult, op1=ALU.add),
                            [self.bps[bank], cpr, y], [y])
                        self.dve(lambda e, pbk=pbk, yv=yv, w0=w0, n=n: e.scalar_tensor_tensor(
                            out=yv[:, 0:n], in0=pbk[:, 0:n], scalar=w0, in1=yv[:, 0:n], op0=ALU.mult, op1=ALU.add),
                            [self.bps[bank], cpr, y], [y])
                    ys.append((y, yv))
                for cc in range(2):
                    sg = sgb[cc]
                    sgv = sg.ap(F32)
                    self.act(lambda e, sgv=sgv, yv=ys[cc][1], n=n: e.activation(out=sgv[:, 0:n], in_=yv[:, 0:n], func=AF.Silu),
                             [ys[cc][0]], [sg])
                    self.op("pool", lambda e, sgv=sgv, yv=ys[2 + cc][1], n=n, cc=cc, ot0=ot0, abv=abv: e.tensor_tensor(
                        out=abv[:, cc, ot0:ot0 + n], in0=sgv[:, 0:n], in1=yv[:, 0:n], op=ALU.mult),
                        [sg, ys[2 + cc][0]], [ab])
            if pend is not None:
                down(*pend)
            pend = (G, wo_r, wov, ab)
        down(*pend)

    def store_out(self, s):
        X, SM = self.X, self.SM
        bm = Bump(self, 0, NPAGES)
        ov = self.out_d[s].rearrange("(t p) d -> p t d", p=128)
        stg = [bm.alloc(4096) for _ in range(3)]
        if not self.final:
            for t in range(NT):
                sr = stg[t % 3]
                self.act(lambda e, t=t, sr=sr: e.activation(out=sr.ap(F32), in_=X[:, t, :], func=AF.Copy), [self.bx[t]], [sr])
                self.P.dma("sp", lambda e, t=t, sr=sr: e.dma_start(out=ov[:, t, :], in_=sr.ap(F32)), sr.semh,
                           reads=flat(sr), writes=())
            for sr in stg:
                self.P.final_wait("sp", sr.semh.sem, sr.semh.cnt)
            return
        gb = bm.alloc(4096)
        gbv = gb.ap(F32)
        self.sload(gb, gbv, self.w["final_norm"].rearrange("(a d) -> a d", a=1).broadcast_to([128, D]))
        junk = bm.alloc(2048)
        bs = self.smb("ss")
        ss, lnv, rstd = SM[:, 0:16], SM[:, 16:32], SM[:, 32:48]
        for t in range(NT):
            self.act(lambda e, t=t: e.activation(out=junk.ap(), in_=X[:, t, :], func=AF.Square, accum_out=ss[:, t:t + 1]),
                     [self.bx[t]], [junk, bs])
        self.act(lambda e: e.activation(out=lnv, in_=ss, func=AF.Ln, scale=1.0 / D, bias=EPS), [bs], [bs])
        self.act(lambda e: e.activation(out=rstd, in_=lnv, func=AF.Exp, scale=-0.5), [bs], [bs])
        for t in range(NT):
            sr = stg[t % 3]
            self.dve(lambda e, t=t, sr=sr: e.scalar_tensor_tensor(out=sr.ap(F32), in0=X[:, t, :], scalar=rstd[:, t:t + 1],
                                                                 in1=gbv, op0=ALU.mult, op1=ALU.mult),
                     [self.bx[t], bs, gb], [sr])
            self.P.dma("sp", lambda e, t=t, sr=sr: e.dma_start(out=ov[:, t, :], in_=sr.ap(F32)), sr.semh,
                       reads=flat(sr), writes=())
        for sr in stg:
            self.P.final_wait("sp", sr.semh.sem, sr.semh.cnt)


_CACHE = {}


def _get_nc(nseq, layers, final):
    key = (nseq, tuple(layers), final)
    if key not in _CACHE:
        _CACHE[key] = KB(nseq, layers, final).nc
    return _CACHE[key]


def kernel(**inputs):
    ncores = 8
    x = np.ascontiguousarray(np.asarray(inputs["x"], dtype=np.float32))
    pos = np.ascontiguousarray(np.asarray(inputs["positions"], dtype=np.int32))
    nseq = x.shape[0] // ncores
    nc = _get_nc(nseq, (0, 1, 2, 3), True)
    cpk = make_cpack()
    wts = {name: np.ascontiguousarray(np.asarray(inputs[name], dtype=np.float32)) for name, _ in WEIGHT_SPECS}
    in_maps = []
    for c in range(ncores):
        m = {"x": x[c * nseq:(c + 1) * nseq], "positions": pos[c * nseq:(c + 1) * nseq], "cpack": cpk}
        m.update(wts)
        in_maps.append(m)
    res = run_bass_kernel_spmd(nc, in_maps, core_ids=list(range(ncores)))
    return np.concatenate([r["out"] for r in res.results], axis=0).astype(np.float32)
```
